# Optimizing a Trainium2 kernel written in Bass

```python
import math
import jax, jax.numpy as jnp
from jax import lax
import numpy as np

D_MODEL = 1024
BATCH = 2
SEQ = 8192
DEPTH = 4

GRID_W = 64
CTX_LEN = 256
D_MIX = D_MODEL
N_GROUPS = 4
GROUP_W = D_MIX // N_GROUPS
HEAD_DIM = 64
N_HEADS_G = GROUP_W // HEAD_DIM
CHUNK = 64
Q_BLOCK = 128
EPS = 1e-6
ROPE_BASE = 10000.0
W_MOD_SCALE = 0.5
DA_QK = HEAD_DIM // 2
GLA_DK = HEAD_DIM // 2
GLA_RANK = 16
GLA_TAU = 16.0
GDN_CONV = 5
M_COLS = 5 * GROUP_W + 4 * N_HEADS_G
A_COLS = 4 * GROUP_W
G_COLS = 2 * N_HEADS_G * GLA_DK + 2 * GROUP_W + 2 * GLA_RANK
D_COLS = 4 * GROUP_W + 4 * N_HEADS_G
P_IN = M_COLS + A_COLS + G_COLS + D_COLS
F32 = jnp.float32

kernel_name = 'hybrid_quad_mixer_diffusion_block'


def _split(u, sizes):
    idx = np.cumsum(sizes)[:-1].tolist()
    return jnp.split(u, idx, axis=-1)


def _rms(x, w):
    xf = x.astype(F32)
    y = xf * lax.rsqrt(jnp.mean(xf * xf, axis=-1, keepdims=True) + EPS) * w.astype(F32)
    return y.astype(x.dtype)


def _l2n(x):
    xf = x.astype(F32)
    return xf * lax.rsqrt(jnp.sum(xf * xf, axis=-1, keepdims=True) + EPS)


def _heads(a):
    B, T, _ = a.shape
    return a.reshape(B, T, N_HEADS_G, -1).transpose(0, 2, 1, 3)


def _merge(a):
    B, H, T, d = a.shape
    return a.transpose(0, 2, 1, 3).reshape(B, T, H * d)


def _head_norm(h, w):
    H, d = h.shape[1], h.shape[-1]
    return _rms(h, w.reshape(H, 1, d))


def _to_chunks(a):
    B, H, T = a.shape[:3]
    a = a.reshape((B, H, T // CHUNK, CHUNK) + a.shape[3:])
    return jnp.moveaxis(a, 2, 0)


def _from_chunks(a):
    a = jnp.moveaxis(a, 0, 2)
    return a.reshape(a.shape[:2] + (a.shape[2] * a.shape[3],) + a.shape[4:])


def _flip_t(a):
    return jnp.flip(a, axis=2)


def _identity(a):
    return a


def _bidir(run, ctx_dirs, lat_dirs, state0):
    y_ctx, y_lat = [], []
    for d in range(2):
        fl = _flip_t if d == 1 else _identity
        yc, st = run(*[fl(a) for a in ctx_dirs[d]], state0)
        yl, _ = run(*[fl(a) for a in lat_dirs[d]], st)
        y_ctx.append(fl(yc))
        y_lat.append(fl(yl))
    return y_ctx[0] + y_ctx[1], y_lat[0] + y_lat[1]


def _axial_rope_tables(rows):
    row = jnp.repeat(jnp.arange(rows), GRID_W).astype(F32)
    col = jnp.tile(jnp.arange(GRID_W), rows).astype(F32)
    half = DA_QK // 2
    inv = jnp.power(ROPE_BASE, -jnp.arange(0, half, 2, dtype=F32) / half)
    def tab(p):
        ang = p[:, None] * inv
        ang = jnp.concatenate([ang, ang], axis=-1)
        return jnp.cos(ang), jnp.sin(ang)
    cr, sr = tab(row)
    cc, sc = tab(col)
    return jnp.concatenate([cr, cc], -1), jnp.concatenate([sr, sc], -1)


def _rot_half(x):
    x1, x2 = jnp.split(x, 2, axis=-1)
    return jnp.concatenate([-x2, x1], axis=-1)


def _rope2d(x, cos, sin):
    half = x.shape[-1] // 2
    rot = jnp.concatenate([_rot_half(x[..., :half]), _rot_half(x[..., half:])], axis=-1)
    return (x * cos + rot * sin).astype(x.dtype)


def _dwconv(x, w):
    K, C = w.shape
    return lax.conv_general_dilated(x, w[:, None, :].astype(x.dtype), window_strides=(1,),
                                    padding=[(K // 2, K // 2)],
                                    dimension_numbers=('NWC', 'WIO', 'NWC'),
                                    feature_group_count=C)


def _mlstm_run(q, k, v, log_i, log_f, state):
    tri = jnp.tril(jnp.ones((CHUNK, CHUNK), bool))
    def step(carry, inp):
        C, n, m = carry
        qc, kc, vc, ic, fc = inp
        b = jnp.cumsum(fc, axis=-1)
        d_ts = jnp.where(tri, b[..., :, None] - b[..., None, :] + ic[..., None, :], -jnp.inf)
        inter = b + m[..., None]
        m_t = jnp.maximum(inter, jnp.max(d_ts, axis=-1))
        s = jnp.einsum('bhtd,bhsd->bhts', qc, kc) * jnp.exp(d_ts - m_t[..., None])
        w_inter = jnp.exp(inter - m_t)
        num = (w_inter[..., None] * jnp.einsum('bhtd,bhde->bhte', qc, C)
               + jnp.einsum('bhts,bhse->bhte', s, vc))
        nq = w_inter * jnp.einsum('bhtd,bhd->bht', qc, n) + jnp.sum(s, axis=-1)
        h = num / jnp.maximum(jnp.abs(nq), jnp.exp(-m_t))[..., None]
        b_last = b[..., -1]
        d_s = b_last[..., None] - b + ic
        m_new = jnp.maximum(b_last + m, jnp.max(d_s, axis=-1))
        w_s = jnp.exp(d_s - m_new[..., None])
        w_c = jnp.exp(b_last + m - m_new)
        C = w_c[..., None, None] * C + jnp.einsum('bhs,bhsd,bhse->bhde', w_s, kc, vc)
        n = w_c[..., None] * n + jnp.einsum('bhs,bhsd->bhd', w_s, kc)
        return (C, n, m_new), h
    state, h = lax.scan(step, state, tuple(_to_chunks(a) for a in (q, k, v, log_i, log_f)))
    return _from_chunks(h), state


def _gla_run(q, k, v, log_a, S):
    tri = jnp.tril(jnp.ones((CHUNK, CHUNK), bool))
    def step(S, inp):
        qc, kc, vc, ac = inp
        b = jnp.cumsum(ac, axis=2)
        rel = jnp.exp(jnp.where(tri[:, :, None], b[:, :, :, None, :] - b[:, :, None, :, :], -jnp.inf))
        A = jnp.einsum('bhtc,bhsc,bhtsc->bhts', qc, kc, rel)
        o = (jnp.einsum('bhtc,bhce->bhte', qc * jnp.exp(b), S)
             + jnp.einsum('bhts,bhse->bhte', A, vc))
        b_last = b[:, :, -1:, :]
        S = (jnp.exp(b_last[:, :, 0])[..., None] * S
             + jnp.einsum('bhsc,bhse->bhce', kc * jnp.exp(b_last - b), vc))
        return S, o
    S, o = lax.scan(step, S, tuple(_to_chunks(a) for a in (q, k, v, log_a)))
    return _from_chunks(o), S


def _gdn_run(q, k, v, g, beta, S):
    qc, kc, vc, gc, bc = (_to_chunks(a) for a in (q, k, v, g, beta))
    gc = jnp.cumsum(gc, axis=-1)
    tri = jnp.tril(jnp.ones((CHUNK, CHUNK), bool))
    strict = jnp.tril(jnp.ones((CHUNK, CHUNK), bool), k=-1)
    decay = jnp.exp(jnp.where(tri, gc[..., :, None] - gc[..., None, :], -jnp.inf))
    kb = kc * bc[..., None]
    lower = jnp.where(strict, jnp.einsum('nbhtd,nbhsd->nbhts', kb, kc) * decay, 0.0)
    eye = jnp.eye(CHUNK, dtype=lower.dtype)
    tmat = lax.linalg.triangular_solve(eye + lower, jnp.broadcast_to(eye, lower.shape),
                                       left_side=True, lower=True, unit_diagonal=True)
    u = tmat @ (vc * bc[..., None])
    w = tmat @ (kb * jnp.exp(gc)[..., None])
    attn = jnp.einsum('nbhtd,nbhsd->nbhts', qc, kc) * decay
    def step(S, inp):
        qi, ki, ui, wi, ai, gi = inp
        v_new = ui - jnp.einsum('bhtc,bhce->bhte', wi, S)
        o = (jnp.einsum('bhtc,bhce->bhte', qi * jnp.exp(gi)[..., None], S)
             + jnp.einsum('bhts,bhse->bhte', ai, v_new))
        g_last = gi[..., -1:]
        S = (jnp.exp(g_last)[..., None] * S
             + jnp.einsum('bhsc,bhse->bhce', ki * jnp.exp(g_last - gi)[..., None], v_new))
        return S, o
    S, o = lax.scan(step, S, (qc, kc, u, w, attn, gc))
    return _from_chunks(o), S


def _diff_attend(q, k, v, lam):
    B, H, _, T, dq = q.shape
    nb = T // Q_BLOCK
    qb = q.reshape(B, H, 2, nb, Q_BLOCK, dq).transpose(3, 0, 1, 2, 4, 5)
    scale = dq ** -0.5
    def one(qblk):
        s = jnp.einsum('bhjqd,bhjkd->bhjqk', qblk, k).astype(F32) * scale
        p = jax.nn.softmax(s, axis=-1)
        wts = p[:, :, 0] - lam * p[:, :, 1]
        return jnp.einsum('bhqk,bhkd->bhqd', wts.astype(v.dtype), v)
    o = lax.map(one, qb)
    return o.transpose(1, 2, 0, 3, 4).reshape(B, H, T, -1)


def _mlstm_mixer(uc, ux, b_i, b_f, norm_w, need_ctx):
    def prep(u):
        B, T, _ = u.shape
        q, k, v, o, z, ig, fg = _split(u, [GROUP_W] * 5 + [2 * N_HEADS_G] * 2)
        q = _heads(q.astype(F32))
        k = _heads(k.astype(F32)) * HEAD_DIM ** -0.5
        v = _heads(v.astype(F32))
        gshape = (B, T, 2, N_HEADS_G)
        log_i = (ig.astype(F32).reshape(gshape) + b_i).transpose(2, 0, 3, 1)
        log_f = jax.nn.log_sigmoid(fg.astype(F32).reshape(gshape) + b_f).transpose(2, 0, 3, 1)
        return [(q, k, v, log_i[d], log_f[d]) for d in range(2)], (o, z)
    ctx_dirs, gates_c = prep(uc)
    lat_dirs, gates_x = prep(ux)
    B = ux.shape[0]
    state0 = (jnp.zeros((B, N_HEADS_G, HEAD_DIM, HEAD_DIM), F32),
              jnp.zeros((B, N_HEADS_G, HEAD_DIM), F32),
              jnp.zeros((B, N_HEADS_G), F32))
    hc, hx = _bidir(_mlstm_run, ctx_dirs, lat_dirs, state0)
    def finish(h, gates, dtype):
        o, z = gates
        y = (_merge(_head_norm(h, norm_w)) * jax.nn.sigmoid(o.astype(F32))
             * jax.nn.silu(z.astype(F32)))
        return y.astype(dtype)
    yc = finish(hc, gates_c, uc.dtype) if need_ctx else None
    return yc, finish(hx, gates_x, ux.dtype)


def _diff_mixer(uc, ux, cos, sin, q_norm, k_norm, lam, norm_w, lam_init, need_ctx):
    def prep(u):
        B, T, _ = u.shape
        q, k, v, z = _split(u, [GROUP_W] * 4)
        def parts(a):
            return a.reshape(B, T, N_HEADS_G, 2, DA_QK).transpose(0, 2, 3, 1, 4)
        return _rms(parts(q), q_norm), _rms(parts(k), k_norm), _heads(v), z
    qc, kc, vc, zc = prep(uc)
    qx, kx, vx, zx = prep(ux)
    qx = _rope2d(qx, cos, sin)
    kx = _rope2d(kx, cos, sin)
    lf = lam.astype(F32)
    lam_full = jnp.exp(jnp.sum(lf[0] * lf[1])) - jnp.exp(jnp.sum(lf[2] * lf[3])) + lam_init
    def finish(o, z, dtype):
        y = _merge(_head_norm(o, norm_w)).astype(F32) * (1.0 - lam_init) * jax.nn.silu(z.astype(F32))
        return y.astype(dtype)
    ox = _diff_attend(qx, jnp.concatenate([kx, kc], axis=3), jnp.concatenate([vx, vc], axis=2), lam_full)
    yx = finish(ox, zx, ux.dtype)
    yc = finish(_diff_attend(qc, kc, vc, lam_full), zc, uc.dtype) if need_ctx else None
    return yc, yx


def _gla_mixer(uc, ux, w_up, b_gk, norm_w, need_ctx):
    def prep(u):
        B, T, _ = u.shape
        q, k, v, z, r = _split(u, [N_HEADS_G * GLA_DK] * 2 + [GROUP_W] * 2 + [2 * GLA_RANK])
        q = _heads(q.astype(F32)) * GLA_DK ** -0.5
        k = _heads(k.astype(F32))
        v = _heads(v.astype(F32))
        r = r.astype(F32).reshape(B, T, 2, GLA_RANK)
        gk = jnp.einsum('btdr,drk->dbtk', r, w_up.astype(F32)) + b_gk.astype(F32)[:, None, None, :]
        log_a = jax.nn.log_sigmoid(gk) / GLA_TAU
        log_a = log_a.reshape(2, B, T, N_HEADS_G, GLA_DK).transpose(0, 1, 3, 2, 4)
        return [(q, k, v, log_a[d]) for d in range(2)], z
    ctx_dirs, zc = prep(uc)
    lat_dirs, zx = prep(ux)
    B = ux.shape[0]
    state0 = jnp.zeros((B, N_HEADS_G, GLA_DK, HEAD_DIM), F32)
    oc, ox = _bidir(_gla_run, ctx_dirs, lat_dirs, state0)
    def finish(o, z, dtype):
        return (_merge(_head_norm(o, norm_w)) * jax.nn.silu(z.astype(F32))).astype(dtype)
    yc = finish(oc, zc, uc.dtype) if need_ctx else None
    return yc, finish(ox, zx, ux.dtype)


def _gdn_mixer(uc, ux, conv_w, a_log, dt_bias, norm_w, need_ctx):
    def prep(u):
        B, T, _ = u.shape
        qkv, z, bt, a = _split(u, [3 * GROUP_W, GROUP_W, 2 * N_HEADS_G, 2 * N_HEADS_G])
        qkv = jax.nn.silu(_dwconv(qkv, conv_w))
        q, k, v = jnp.split(qkv, 3, axis=-1)
        q = _l2n(_heads(q)) * HEAD_DIM ** -0.5
        k = _l2n(_heads(k))
        v = _heads(v).astype(F32)
        gshape = (B, T, 2, N_HEADS_G)
        beta = jax.nn.sigmoid(bt.astype(F32).reshape(gshape)).transpose(2, 0, 3, 1)
        dt = jax.nn.softplus(a.astype(F32).reshape(gshape) + dt_bias).transpose(2, 0, 3, 1)
        g = -jnp.exp(a_log.astype(F32))[:, None, :, None] * dt
        return [(q, k, v, g[d], beta[d]) for d in range(2)], z
    ctx_dirs, zc = prep(uc)
    lat_dirs, zx = prep(ux)
    B = ux.shape[0]
    state0 = jnp.zeros((B, N_HEADS_G, HEAD_DIM, HEAD_DIM), F32)
    oc, ox = _bidir(_gdn_run, ctx_dirs, lat_dirs, state0)
    def finish(o, z, dtype):
        return (_merge(_head_norm(o, norm_w)) * jax.nn.silu(z.astype(F32))).astype(dtype)
    yc = finish(oc, zc, uc.dtype) if need_ctx else None
    return yc, finish(ox, zx, ux.dtype)


def setup_inputs(seed: int = 0) -> dict:
    key = jax.random.key(seed)
    ks = jax.random.split(key, 24)
    H = N_HEADS_G
    def nrm(k, shape, s=1.0):
        return jax.random.normal(k, shape, F32) * s
    dt = jnp.exp(jax.random.uniform(ks[20], (DEPTH, 2, H), F32)
                 * (math.log(0.1) - math.log(0.001)) + math.log(0.001))
    return {
        'x': nrm(ks[0], (BATCH, SEQ, D_MODEL)),
        'c': nrm(ks[1], (BATCH, D_MODEL)),
        'ctx': nrm(ks[2], (BATCH, CTX_LEN, D_MODEL)),
        'c_ctx': nrm(ks[3], (D_MODEL,)),
        'norm_w': 1.0 + nrm(ks[4], (DEPTH, D_MODEL), 0.02),
        'w_mod': nrm(ks[5], (DEPTH, D_MODEL, 3 * D_MODEL), W_MOD_SCALE * D_MODEL ** -0.5),
        'b_mod': nrm(ks[6], (DEPTH, 3 * D_MODEL), 0.02),
        'w_in': nrm(ks[7], (DEPTH, D_MODEL, P_IN), D_MODEL ** -0.5),
        'w_out': nrm(ks[8], (DEPTH, D_MIX, D_MODEL), D_MIX ** -0.5),
        'mlstm_b_i': nrm(ks[9], (DEPTH, 2, H), 0.1),
        'mlstm_b_f': jnp.linspace(3.0, 6.0, H, dtype=F32) + nrm(ks[10], (DEPTH, 2, H), 0.1),
        'mlstm_norm': 1.0 + nrm(ks[11], (DEPTH, GROUP_W), 0.02),
        'diff_q_norm': 1.0 + nrm(ks[12], (DEPTH, DA_QK), 0.02),
        'diff_k_norm': 1.0 + nrm(ks[13], (DEPTH, DA_QK), 0.02),
        'diff_lambda': nrm(ks[14], (DEPTH, 4, DA_QK), 0.1),
        'diff_norm': 1.0 + nrm(ks[15], (DEPTH, GROUP_W), 0.02),
        'gla_w_up': nrm(ks[16], (DEPTH, 2, GLA_RANK, H * GLA_DK), GLA_RANK ** -0.5),
        'gla_b': nrm(ks[17], (DEPTH, 2, H * GLA_DK), 0.1),
        'gla_norm': 1.0 + nrm(ks[18], (DEPTH, GROUP_W), 0.02),
        'gdn_conv': nrm(ks[19], (DEPTH, GDN_CONV, 3 * GROUP_W), GDN_CONV ** -0.5),
        'gdn_a_log': jnp.log(jax.random.uniform(ks[21], (DEPTH, 2, H), F32, 1.0, 16.0)),
        'gdn_dt_bias': dt + jnp.log(-jnp.expm1(-dt)),
        'gdn_norm': 1.0 + nrm(ks[22], (DEPTH, GROUP_W), 0.02),
    }


def reference(x, c, ctx, c_ctx, norm_w, w_mod, b_mod, w_in, w_out, mlstm_b_i, mlstm_b_f, mlstm_norm,
              diff_q_norm, diff_k_norm, diff_lambda, diff_norm, gla_w_up, gla_b, gla_norm,
              gdn_conv, gdn_a_log, gdn_dt_bias, gdn_norm):
    n_lat = x.shape[1]
    rows = n_lat // GRID_W
    cos, sin = _axial_rope_tables(rows)
    for l in range(DEPTH):
        need_ctx = l < DEPTH - 1
        mod = jax.nn.silu(c) @ w_mod[l] + b_mod[l]
        mod_c = jax.nn.silu(c_ctx) @ w_mod[l] + b_mod[l]
        sh, sc, gt = jnp.split(mod, 3, axis=-1)
        sh_c, sc_c, gt_c = jnp.split(mod_c, 3, axis=-1)
        hx = _rms(x, norm_w[l]) * (1.0 + sc[:, None, :]) + sh[:, None, :]
        hc = _rms(ctx, norm_w[l]) * (1.0 + sc_c) + sh_c
        ux = _split(hx @ w_in[l], [M_COLS, A_COLS, G_COLS, D_COLS])
        uc = _split(hc @ w_in[l], [M_COLS, A_COLS, G_COLS, D_COLS])
        lam_init = 0.8 - 0.6 * math.exp(-0.3 * l)
        outs = [
            _mlstm_mixer(uc[0], ux[0], mlstm_b_i[l], mlstm_b_f[l], mlstm_norm[l], need_ctx),
            _diff_mixer(uc[1], ux[1], cos, sin, diff_q_norm[l], diff_k_norm[l], diff_lambda[l],
                        diff_norm[l], lam_init, need_ctx),
            _gla_mixer(uc[2], ux[2], gla_w_up[l], gla_b[l], gla_norm[l], need_ctx),
            _gdn_mixer(uc[3], ux[3], gdn_conv[l], gdn_a_log[l], gdn_dt_bias[l], gdn_norm[l], need_ctx),
        ]
        yx = jnp.concatenate([o[1] for o in outs], axis=-1)
        x = x + gt[:, None, :] * (yx @ w_out[l])
        if need_ctx:
            yc = jnp.concatenate([o[0] for o in outs], axis=-1)
            ctx = ctx + gt_c * (yc @ w_out[l])
    return x
```

```python
import contextlib
import math
import numpy as np
import concourse.bass as bass
import concourse.mybir as mybir
from concourse.bass_utils import run_bass_kernel_spmd

F32 = mybir.dt.float32
BF16 = mybir.dt.bfloat16
AF = mybir.ActivationFunctionType
ALU = mybir.AluOpType
AX = mybir.AxisListType

D = 1024
DEPTH = 4
NCTX_T = 2
EPS = 1e-6
MC, AC, GC, DC = 324, 256, 224, 260
WCOLS = MC + AC + GC + DC


class Buf:
    __slots__ = ("w", "r")

    def __init__(self):
        self.w = None
        self.r = []


class Eng:
    def __init__(self, name, e, sem, inorder=False, step=1):
        self.name, self.e, self.sem, self.inorder, self.step = name, e, sem, inorder, step
        self.cnt = 0
        self.seen = {}


class TT:
    def __init__(self, t, nkeys=None, excl=False):
        self.t = t
        self.nkeys = nkeys
        self.excl = excl
        self.bufs = [Buf() for _ in range(nkeys or 1)]

    def __call__(self, keys, *idx):
        ap = self.t[idx] if idx else self.t[:]
        return Ref(self, keys, ap)

    def kb(self, keys):
        if self.nkeys is None or keys is None:
            return self.bufs
        if isinstance(keys, int):
            return [self.bufs[keys]]
        return [self.bufs[i] for i in keys]


class Ref:
    def __init__(self, tt, keys, ap):
        self.tt, self.keys, self.ap = tt, keys, ap

    def bufs(self):
        return self.tt.kb(self.keys)

    def bc(self, shape):
        return Ref(self.tt, self.keys, self.ap.broadcast_to(shape))

    def cast(self, dt):
        return Ref(self.tt, self.keys, self.ap.bitcast(dt))


class KB:
    def __init__(self, nc, st):
        self.nc, self.st = nc, st
        mk = lambda n: st.enter_context(nc.semaphore(n))
        self.pe = Eng("pe", nc.tensor, mk("s_pe"), inorder=True)
        self.act = Eng("act", nc.scalar, mk("s_act"))
        self.dve = Eng("dve", nc.vector, mk("s_dve"))
        self.pool = Eng("pool", nc.gpsimd, mk("s_pool"))
        self.sp = Eng("sp", nc.sync, mk("s_sp"))
        self.engs = [self.pe, self.act, self.dve, self.pool, self.sp]
        self.streams = []
        self.ninst = 0

    def stream(self, name):
        s = Eng(name, None, self.st.enter_context(self.nc.semaphore(name)), step=16)
        self.streams.append(s)
        return s

    def sb(self, name, shape, dt, nkeys=None, st=None):
        self.nsb = getattr(self, "nsb", 0) + 1
        t = (st or self.st).enter_context(self.nc.sbuf_tensor(f"{name}_{self.nsb}", shape, dt))
        return TT(t, nkeys)

    def ps(self, name, shape, dt):
        return TT(self.st.enter_context(self.nc.psum_tensor(name, shape, dt)), excl=True)

    def _deps(self, eng, R, W):
        need = {}
        for r in R:
            for b in r.bufs():
                if b.w is not None:
                    e, c = b.w
                    if need.get(e, 0) < c:
                        need[e] = c
                if r.tt.excl:
                    for e, c in b.r:
                        if e is not eng and need.get(e, 0) < c:
                            need[e] = c
        for w in W:
            for b in w.bufs():
                if b.w is not None:
                    e, c = b.w
                    if need.get(e, 0) < c:
                        need[e] = c
                for e, c in b.r:
                    if need.get(e, 0) < c:
                        need[e] = c
        for e, c in need.items():
            if e is eng and eng.inorder:
                continue
            if eng.seen.get(e, 0) >= c:
                continue
            eng.e.wait_ge(e.sem, c)
            eng.seen[e] = c

    def _mark(self, tag, R, W):
        e = tag[0]
        for r in R:
            for b in r.bufs():
                b.r = [x for x in b.r if x[0] is not e]
                b.r.append(tag)
        for w in W:
            for b in w.bufs():
                b.w = tag
                b.r = []

    def op(self, eng, fn, R, W):
        self._deps(eng, R, W)
        ins = fn()
        eng.cnt += 1
        ins.then_inc(eng.sem, 1)
        self._mark((eng, eng.cnt), R, W)
        self.ninst += 1

    def dma(self, q, stream, out, in_, **kw):
        R = [in_] if isinstance(in_, Ref) else []
        W = [out] if isinstance(out, Ref) else []
        self._deps(q, R, W)
        o = out.ap if isinstance(out, Ref) else out
        i = in_.ap if isinstance(in_, Ref) else in_
        ins = q.e.dma_start(out=o, in_=i, **kw)
        stream.cnt += 16
        ins.then_inc(stream.sem, 16)
        self._mark((stream, stream.cnt), R, W)
        self.ninst += 1

    def barrier(self):
        tgt = [(e, e.cnt) for e in self.engs + self.streams if e.cnt > 0]
        for eng in self.engs:
            for e, c in tgt:
                if e is eng and eng.inorder:
                    continue
                if eng.seen.get(e, 0) >= c:
                    continue
                eng.e.wait_ge(e.sem, c)
                eng.seen[e] = c

    def mm(self, out, lhsT, rhs, start=True, stop=True):
        self.op(self.pe, lambda: self.nc.tensor.matmul(out.ap, lhsT=lhsT.ap, rhs=rhs.ap, start=start, stop=stop),
                [lhsT, rhs], [out])

    def tr(self, out, in_, ident):
        self.op(self.pe, lambda: self.nc.tensor.transpose(out=out.ap, in_=in_.ap, identity=ident.ap),
                [in_, ident], [out])

    def actf(self, out, in_, func, bias=None, scale=None, accum=None, eng=None):
        R = [in_]
        W = [out]
        kw = {}
        if bias is not None:
            if isinstance(bias, Ref):
                R.append(bias)
                kw["bias"] = bias.ap
            else:
                kw["bias"] = bias
        if scale is not None:
            if isinstance(scale, Ref):
                R.append(scale)
                kw["scale"] = scale.ap
            else:
                kw["scale"] = scale
        if accum is not None:
            W.append(accum)
            kw["accum_out"] = accum.ap
        self.op(self.act, lambda: self.nc.scalar.activation(out=out.ap, in_=in_.ap, func=func, **kw), R, W)

    def _ve(self, eng):
        return self.dve if eng is None else eng

    def tt(self, out, in0, in1, op, eng=None):
        eng = self._ve(eng)
        self.op(eng, lambda: eng.e.tensor_tensor(out=out.ap, in0=in0.ap, in1=in1.ap, op=op), [in0, in1], [out])

    def ts(self, out, in0, s1, op0, s2=None, op1=None, eng=None, accum=None):
        eng = self._ve(eng)
        R = [in0]
        W = [out]
        a1 = s1
        a2 = s2
        if isinstance(s1, Ref):
            R.append(s1)
            a1 = s1.ap
        if isinstance(s2, Ref):
            R.append(s2)
            a2 = s2.ap
        kw = {}
        if op1 is not None:
            kw["op1"] = op1
        if accum is not None:
            W.append(accum)
            kw["accum_out"] = accum.ap
        self.op(eng, lambda: eng.e.tensor_scalar(out=out.ap, in0=in0.ap, scalar1=a1, scalar2=a2, op0=op0, **kw), R, W)

    def stt(self, out, in0, s, in1, op0, op1):
        R = [in0, in1]
        a = s
        if isinstance(s, Ref):
            R.append(s)
            a = s.ap
        self.op(self.dve, lambda: self.nc.vector.scalar_tensor_tensor(out=out.ap, in0=in0.ap, scalar=a, in1=in1.ap,
                                                                      op0=op0, op1=op1), R, [out])

    def cp(self, out, in_, eng=None):
        eng = self._ve(eng)
        if eng is self.act:
            self.op(eng, lambda: self.nc.scalar.copy(out=out.ap, in_=in_.ap), [in_], [out])
        else:
            self.op(eng, lambda: eng.e.tensor_copy(out=out.ap, in_=in_.ap), [in_], [out])

    def red(self, out, in_, op=ALU.add, axis=AX.X):
        self.op(self.dve, lambda: self.nc.vector.tensor_reduce(out=out.ap, in_=in_.ap, axis=axis, op=op), [in_], [out])

    def recip(self, out, in_):
        self.op(self.dve, lambda: self.nc.vector.reciprocal(out=out.ap, in_=in_.ap), [in_], [out])

    def memset(self, out, val, eng=None):
        eng = self._ve(eng)
        self.op(eng, lambda: eng.e.memset(out.ap, val), [], [out])

    def rsqrt(self, out, in_, eps, tmp):
        self.actf(tmp, in_, AF.Sqrt, bias=eps, scale=1.0)
        self.recip(out, tmp)


def const_arrays():
    p = np.arange(128)[:, None]
    f = np.arange(128)[None, :]
    same = (p // 64) == (f // 64)
    m = np.stack([p <= f, p >= f, (p <= f) & same, (p >= f) & same, (p > f) & same, (p < f) & same], axis=1)
    cm = np.ascontiguousarray(m.astype(np.float32))
    ident = np.eye(128, dtype=np.float32)
    return {"cmask": cm, "ident": ident}


class Prog:
    def __init__(self, cfg):
        self.cfg = cfg
        self.NT = cfg["NT"]
        self.T = self.NT * 128
        nc = bass.Bass("TRN2", target_bir_lowering=False)
        self.nc = nc
        self.dr = {}
        self.st = contextlib.ExitStack()
        with self.st:
            self.k = KB(nc, self.st)
            self.build()

    def din(self, name, shape, dt=F32):
        self.dr[name] = self.nc.dram_tensor(name, list(shape), dt, kind="ExternalInput").ap()
        return self.dr[name]

    def dout(self, name, shape, dt=F32):
        self.dr[name] = self.nc.dram_tensor(name, list(shape), dt, kind="ExternalOutput").ap()
        return self.dr[name]

    def build(self):
        k, nc, cfg = self.k, self.nc, self.cfg
        T = self.T
        self.din("cmask", [128, 6, 128])
        self.din("ident", [128, 128])
        self.din("xin", [T, D])
        self.din("c2", [2, D])
        if cfg["pro"]:
            self.din("yprev", [T, D])
            self.din("wout", [D, D])
            self.din("wgt", [D, D])
            self.din("bgt", [1, D])
            self.dout("xout", [T, D])
        if cfg["body"]:
            self.din("wss", [D, 2 * D])
            self.din("bss", [2 * D])
            self.din("normw", [D])
            self.din("wcore", [D, WCOLS])
            self.din("sp_row", [1, 512])
            self.din("convw", [5, 192])
            self.din("wup", [2, 16, 32])
            self.din("gb_row", [1, 64])
            self.din("rope", [self.T - 256, 2, 32])
            self.dout("y", [T, 256])
        self.ld = [k.stream("ld0"), k.stream("ld1"), k.stream("ld2"), k.stream("ld3")]
        self.stq = [k.stream("st0"), k.stream("st1")]
        self.cst = k.stream("cst")
        self.cmask = k.sb("cmask_s", [128, 6, 128], F32)
        self.ident = k.sb("ident_s", [128, 128], F32)
        self.identb = k.sb("identb_s", [128, 128], BF16)
        self.cmaskb = k.sb("cmaskb_s", [128, 6, 128], BF16)
        self.ones = k.sb("ones_s", [128, 128], F32)
        self.epsc = k.sb("eps_s", [128, 1], F32)
        self.cT = k.sb("cT", [128, 8, 2], F32)
        self.scT = k.sb("scT", [128, 8, 2], F32)
        self.bank = [k.ps(f"bank{i}", [128, 512], F32) for i in range(8)]
        k.dma(k.sp, self.cst, self.cmask(None), self.dr["cmask"])
        k.dma(k.sp, self.cst, self.ident(None), self.dr["ident"])
        for r in range(2):
            k.dma(k.sp, self.cst, self.cT(None, slice(None), slice(None), r),
                  self.dr["c2"][r].rearrange("(c p) -> p c", p=128), allow_slow_non_contiguous=True)
        k.cp(self.identb(None), self.ident(None))
        k.cp(self.cmaskb(None), self.cmask(None))
        k.memset(self.ones(None), 1.0)
        k.memset(self.epsc(None), EPS)
        k.actf(self.scT(None), self.cT(None), AF.Silu)
        xsrc = self.dr["xin"]
        if cfg["pro"]:
            self.prologue()
            xsrc = self.dr["xout"]
        if cfg["body"]:
            self.body(xsrc)
        k.barrier()

    def prologue(self):
        k, nc, dr = self.k, self.nc, self.dr
        NT = self.NT
        with contextlib.ExitStack() as st:
            wo = k.sb("wo_bf", [128, 8, D], BF16, st=st)
            stg = [k.sb(f"wstg{i}", [128, D], F32, st=st) for i in range(2)]
            screp = k.sb("screp", [128, 128], F32, st=st)
            gt = [k.sb(f"gtrep{i}", [128, D], F32, st=st) for i in range(2)]
            bgt = k.sb("bgt_rep", [128, D], F32, st=st)
            xt = [k.sb(f"p_x{i}", [128, D], F32, st=st) for i in range(2)]
            yt = [k.sb(f"p_y{i}", [128, D], F32, st=st) for i in range(2)]
            ybf = [k.sb(f"p_ybf{i}", [128, D], BF16, st=st) for i in range(2)]
            yT = [k.sb(f"p_yT{i}", [128, 8, 128], BF16, st=st) for i in range(2)]
            xo = [k.sb(f"p_xo{i}", [128, D], F32, st=st) for i in range(2)]
            k.dma(k.sp, self.cst, bgt(None), dr["bgt"].partition_broadcast(128).rearrange("p o n -> p (o n)"))
            wv = dr["wout"].rearrange("(c p) n -> p c n", p=128)
            gv = dr["wgt"].rearrange("(c p) n -> p c n", p=128)
            for kc in range(8):
                s = stg[kc % 2]
                k.dma(k.sp, self.ld[kc % 2], s(None), wv[:, kc, :])
                k.cp(wo(None, slice(None), kc, slice(None)), s(None), eng=(k.act if kc % 2 else k.dve))
            for r in range(2):
                for kc in range(8):
                    s = stg[kc % 2]
                    k.dma(k.sp, self.ld[kc % 2], s(None), gv[:, kc, :])
                    k.cp(screp(None), self.scT(None, slice(None), kc, slice(r, r + 1)).bc([128, 128]))
                    for nb in range(2):
                        k.mm(self.bank[nb](None), screp(None), s(None, slice(None), slice(nb * 512, nb * 512 + 512)),
                             start=(kc == 0), stop=(kc == 7))
                for nb in range(2):
                    sl = slice(nb * 512, nb * 512 + 512)
                    k.tt(gt[r](None, slice(None), sl), self.bank[nb](None), bgt(None, slice(None), sl), ALU.add)
            xin_v = dr["xin"].rearrange("(n p) d -> n p d", p=128)
            yp_v = dr["yprev"].rearrange("(n p) d -> n p d", p=128)
            xo_v = dr["xout"].rearrange("(n p) d -> n p d", p=128)
            for n in range(NT):
                i = n % 2
                isl = 0 if n >= NCTX_T else 1
                k.dma(k.sp, self.ld[i], xt[i](None), xin_v[n])
                k.dma(k.sp, self.ld[2 + i], yt[i](None), yp_v[n])
                k.cp(ybf[i](None), yt[i](None), eng=k.pool)
                pb = self.bank[2 + i].cast(BF16) if False else None
                tb = self.bank[2 + i]
                for kc in range(8):
                    k.tr(Ref(tb, None, tb.t[:].bitcast(BF16)[:, kc * 128:(kc + 1) * 128]),
                         ybf[i](None, slice(None), slice(kc * 128, kc * 128 + 128)), self.identb(None))
                k.cp(yT[i](None), Ref(tb, None, tb.t[:].bitcast(BF16)[:, 0:1024].rearrange("p (c t) -> p c t", c=8)),
                     eng=k.act)
                for nb in range(2):
                    pbk = self.bank[4 + 2 * i + nb]
                    for kc in range(8):
                        k.mm(pbk(None), yT[i](None, slice(None), kc, slice(None)),
                             wo(None, slice(None), kc, slice(nb * 512, nb * 512 + 512)), start=(kc == 0), stop=(kc == 7))
                    sl = slice(nb * 512, nb * 512 + 512)
                    k.tt(xo[i](None, slice(None), sl), pbk(None), gt[isl](None, slice(None), sl), ALU.mult)
                k.tt(xo[i](None), xo[i](None), xt[i](None), ALU.add, eng=k.pool)
                k.dma(k.sp, self.stq[i], xo_v[n], xo[i](None))
            k.barrier()

    SPO = dict(mbi=0, mbf=2, ddt=4, dal=6, mnorm=8, anorm=72, gnorm=136, dnorm=200, qn=264, kn=296, lam=328,
               gb=456)
    SPW = 520

    def body(self, xsrc):
        k, nc, dr, cfg = self.k, self.nc, self.dr, self.cfg
        NT = self.NT
        self.xsrc_v = xsrc.rearrange("(n p) d -> n p d", p=128)
        st = self.st
        self.sprep = k.sb("sprep", [128, 640], F32)
        k.dma(k.sp, self.cst, self.sprep(None, slice(None), slice(0, 512)),
              dr["sp_row"].partition_broadcast(128).rearrange("p o n -> p (o n)"))
        self.modT = k.sb("modT", [128, 16, 2], F32)
        self.Gm = k.sb("Gm", [128, 8, 2], F32)
        self.wbf = k.sb("wbf", [128, 8, WCOLS], BF16)
        self.y_v = dr["y"].rearrange("(n p) c -> p n c", p=128)
        with contextlib.ExitStack() as s2:
            wss = k.sb("wss_s", [128, 8, 2 * D], F32, st=s2)
            bssT = k.sb("bssT", [128, 16], F32, st=s2)
            nwT = k.sb("nwT", [128, 8], F32, st=s2)
            stg = [k.sb(f"b_wstg{i}", [128, WCOLS], F32, st=s2) for i in range(2)]
            wv = dr["wss"].rearrange("(c p) n -> p c n", p=128)
            for kc in range(8):
                k.dma(k.sp, self.ld[kc % 4], wss(None, slice(None), kc, slice(None)), wv[:, kc, :])
            k.dma(k.sp, self.cst, bssT(None), dr["bss"].rearrange("(j p) -> p j", p=128), allow_slow_non_contiguous=True)
            k.dma(k.sp, self.cst, nwT(None), dr["normw"].rearrange("(j p) -> p j", p=128), allow_slow_non_contiguous=True)
            for j in range(16):
                for kc in range(8):
                    k.mm(self.bank[0](None, slice(None), slice(2 * j, 2 * j + 2)),
                         wss(None, slice(None), kc, slice(j * 128, j * 128 + 128)), self.scT(None, slice(None), kc, slice(None)),
                         start=(kc == 0), stop=(kc == 7))
            k.tt(self.modT(None), Ref(self.bank[0], None, self.bank[0].t[:, 0:32].rearrange("p (j r) -> p j r", r=2)),
                 Ref(bssT, None, bssT.t[:].unsqueeze(2).broadcast_to([128, 16, 2])), ALU.add)
            k.ts(self.Gm(None), self.modT(None, slice(None), slice(8, 16), slice(None)), 1.0, ALU.add)
            k.tt(self.Gm(None), self.Gm(None), Ref(nwT, None, nwT.t[:].unsqueeze(2).broadcast_to([128, 8, 2])), ALU.mult)
            wc = dr["wcore"].rearrange("(c p) n -> p c n", p=128)
            for kc in range(8):
                s = stg[kc % 2]
                k.dma(k.sp, self.ld[kc % 2], s(None), wc[:, kc, :])
                k.cp(self.wbf(None, slice(None), kc, slice(None)), s(None), eng=(k.act if kc % 2 else k.dve))
            k.barrier()
        for m in cfg["mixers"]:
            with contextlib.ExitStack() as s2:
                getattr(self, "mix_" + m)(s2)
                k.barrier()

    def groups(self):
        g = [[0, 1]]
        n = 2
        while n < self.NT:
            g.append(list(range(n, min(n + 4, self.NT))))
            n += 4
        return g

    def proj_pass(self, s2, c0, ncols, tok_evac, dspecs, d_evac):
        k = self.k
        xt = [k.sb(f"pp_x{i}", [128, D], F32, st=s2) for i in range(2)]
        xn = [k.sb(f"pp_xn{i}", [128, D], F32, st=s2) for i in range(2)]
        ss = [k.sb(f"pp_ss{i}", [128, 1], F32, st=s2) for i in range(2)]
        sq = [k.sb(f"pp_sq{i}", [128, 1], F32, st=s2) for i in range(2)]
        rs = [k.sb(f"pp_rs{i}", [128, 1], F32, st=s2) for i in range(2)]
        hT = [k.sb(f"pp_hT{i}", [128, 8, 512], BF16, st=s2) for i in range(2)]
        for gi, g in enumerate(self.groups()):
            h = hT[gi % 2]
            r = 1 if gi == 0 else 0
            for ti, n in enumerate(g):
                i = n % 2
                k.dma(k.sp, self.ld[i], xt[i](None), self.xsrc_v[n])
                k.tt(xn[i](None), xt[i](None), xt[i](None), ALU.mult, eng=k.pool)
                k.red(ss[i](None), xn[i](None))
                k.actf(sq[i](None), ss[i](None), AF.Sqrt, bias=self.epsc(None), scale=1.0 / D)
                k.recip(rs[i](None), sq[i](None))
                k.ts(xn[i](None), xt[i](None), rs[i](None), ALU.mult)
                for kc in range(8):
                    bk = self.bank[2 * i + kc // 4]
                    k.tr(bk(None, slice(None), slice((kc % 4) * 128, (kc % 4) * 128 + 128)),
                         xn[i](None, slice(None), slice(kc * 128, kc * 128 + 128)), self.ident(None))
                for kc in range(8):
                    bk = self.bank[2 * i + kc // 4]
                    src = bk(None, slice(None), slice((kc % 4) * 128, (kc % 4) * 128 + 128))
                    dst = h(None, slice(None), kc, slice(ti * 128, ti * 128 + 128))
                    if kc % 2:
                        k.actf(dst, src, AF.Identity, bias=self.modT(None, slice(None), kc, slice(r, r + 1)),
                               scale=self.Gm(None, slice(None), kc, slice(r, r + 1)))
                    else:
                        k.ts(dst, src, self.Gm(None, slice(None), kc, slice(r, r + 1)), ALU.mult,
                             self.modT(None, slice(None), kc, slice(r, r + 1)), ALU.add)
            ntok = len(g) * 128
            for ti, n in enumerate(g):
                bk = self.bank[4 + n % 2]
                for kc in range(8):
                    k.mm(bk(None, slice(None), slice(0, ncols)), h(None, slice(None), kc, slice(ti * 128, ti * 128 + 128)),
                         self.wbf(None, slice(None), kc, slice(c0, c0 + ncols)), start=(kc == 0), stop=(kc == 7))
                tok_evac(n, bk)
            for di, (dc0, dn) in enumerate(dspecs):
                bk = self.bank[6 + di % 2]
                for kc in range(8):
                    k.mm(bk(None, slice(0, dn), slice(0, ntok)), self.wbf(None, slice(None), kc, slice(dc0, dc0 + dn)),
                         h(None, slice(None), kc, slice(0, ntok)), start=(kc == 0), stop=(kc == 7))
                d_evac(di, g[0] * 128, ntok, bk)

    def log_sigmoid(self, out, in_, nbias, tmp):
        k = self.k
        k.actf(tmp, in_, AF.Exp, bias=nbias, scale=-1.0)
        k.actf(tmp, tmp, AF.Ln, bias=1.0, scale=1.0)
        k.ts(out, tmp, -1.0, ALU.mult)

    def head_norm_out(self, s2, oacc, gate, woff, mi, extra_scale=None):
        k = self.k
        NT = self.NT
        tmp = k.sb("hn_tmp", [128, NT, 64], F32, st=s2)
        ssum = k.sb("hn_ss", [128, NT], F32, st=s2)
        rstd = k.sb("hn_rs", [128, NT], F32, st=s2)
        k.tt(tmp(None), oacc(None), oacc(None), ALU.mult)
        k.red(ssum(None), tmp(None))
        k.actf(ssum(None), ssum(None), AF.Sqrt, bias=self.epsc(None), scale=1.0 / 64)
        k.recip(rstd(None), ssum(None))
        if extra_scale is not None:
            k.ts(rstd(None), rstd(None), extra_scale, ALU.mult)
        k.tt(tmp(None), oacc(None), Ref(rstd, None, rstd.t[:].unsqueeze(2).broadcast_to([128, NT, 64])), ALU.mult)
        k.tt(tmp(None), tmp(None),
             Ref(self.sprep, None, self.sprep.t[:, woff:woff + 64].unsqueeze(1).broadcast_to([128, NT, 64])), ALU.mult)
        k.tt(tmp(None), tmp(None), gate(None), ALU.mult)
        k.dma(k.sp, self.stq[0], self.y_v[:, :, mi * 64:mi * 64 + 64], tmp(None))

    def dir_order(self, d):
        if d == 0:
            return list(range(self.NT))
        return [1, 0] + list(range(self.NT - 1, 1, -1))

    def mix_mlstm(self, s2):
        k = self.k
        NT, T = self.NT, self.T
        O = self.SPO
        qT = k.sb("m_qT", [64, T], BF16, st=s2)
        kT = k.sb("m_kT", [64, T], BF16, st=s2)
        ktok = k.sb("m_ktok", [128, NT, 64], BF16, st=s2)
        v1 = k.sb("m_v1", [128, NT, 65], F32, st=s2)
        gate = k.sb("m_gate", [128, NT, 64], F32, st=s2)
        graw = k.sb("m_graw", [128, NT, 4], F32, st=s2)
        oacc = k.sb("m_oacc", [128, NT, 64], F32, st=s2)
        t1 = [k.sb(f"m_t1{i}", [128, 64], F32, st=s2) for i in range(2)]
        t2 = [k.sb(f"m_t2{i}", [128, 64], F32, st=s2) for i in range(2)]
        k.memset(v1(None, slice(None), slice(None), slice(64, 65)), 1.0)

        def tok_evac(n, bk):
            i = n % 2
            k.actf(ktok(None, slice(None), n, slice(None)), bk(None, slice(None), slice(64, 128)), AF.Copy, scale=0.125)
            k.cp(v1(None, slice(None), n, slice(0, 64)), bk(None, slice(None), slice(128, 192)))
            k.actf(t1[i](None), bk(None, slice(None), slice(192, 256)), AF.Sigmoid)
            k.actf(t2[i](None), bk(None, slice(None), slice(256, 320)), AF.Silu)
            k.tt(gate(None, slice(None), n, slice(None)), t1[i](None), t2[i](None), ALU.mult)
            k.cp(graw(None, slice(None), n, slice(None)), bk(None, slice(None), slice(320, 324)))

        def d_evac(di, t0, ntok, bk):
            if di == 0:
                k.cp(qT(None, slice(None), slice(t0, t0 + ntok)), bk(None, slice(0, 64), slice(0, ntok)))
            else:
                k.actf(kT(None, slice(None), slice(t0, t0 + ntok)), bk(None, slice(0, 64), slice(0, ntok)), AF.Copy, scale=0.125)

        with contextlib.ExitStack() as s3:
            self.proj_pass(s3, 0, MC, tok_evac, [(0, 64), (64, 64)], d_evac)
            k.barrier()
        nb = k.sb("m_nb", [128, 2], F32, st=s2)
        k.ts(nb(None), self.sprep(None, slice(None), slice(O["mbf"], O["mbf"] + 2)), -1.0, ALU.mult)
        ipre = k.sb("m_ipre", [128, 2, NT], F32, st=s2)
        flog = k.sb("m_flog", [128, 2, NT], F32, st=s2)
        tmpg = k.sb("m_tmpg", [128, NT], F32, st=s2)
        rowsc = k.sb("m_rowsc", [128, 2, NT], F32, st=s2)
        vsc = k.sb("m_vsc", [128, 2, NT], F32, st=s2)
        csc = k.sb("m_csc", [64, 2, NT], F32, st=s2)
        for d in range(2):
            k.ts(ipre(None, slice(None), d, slice(None)), graw(None, slice(None), slice(None), d),
                 self.sprep(None, slice(None), slice(O["mbi"] + d, O["mbi"] + d + 1)), ALU.add)
            self.log_sigmoid(flog(None, slice(None), d, slice(None)), graw(None, slice(None), slice(None), 2 + d),
                             nb(None, slice(None), slice(d, d + 1)), tmpg(None))
            bk = self.bank[d]
            k.mm(bk(None, slice(None), slice(0, NT)), self.cmask(None, slice(None), d, slice(None)),
                 flog(None, slice(None), d, slice(None)))
            k.mm(bk(None, slice(0, 64), slice(128, 128 + NT)), self.ones(None, slice(None), slice(0, 64)),
                 flog(None, slice(None), d, slice(None)))
            k.actf(rowsc(None, slice(None), d, slice(None)), bk(None, slice(None), slice(0, NT)), AF.Exp)
            k.tt(tmpg(None), ipre(None, slice(None), d, slice(None)), bk(None, slice(None), slice(0, NT)), ALU.subtract)
            k.actf(vsc(None, slice(None), d, slice(None)), tmpg(None), AF.Exp)
            k.actf(csc(None, slice(None), d, slice(None)), bk(None, slice(0, 64), slice(128, 128 + NT)), AF.Exp)
        S32 = k.sb("m_S32", [64, 65], F32, st=s2)
        Sbf = k.sb("m_Sbf", [64, 65], BF16, st=s2)
        vh = [k.sb(f"m_vh{i}", [128, 65], BF16, st=s2) for i in range(2)]
        Sm = [k.sb(f"m_Sm{i}", [128, 128], BF16, st=s2) for i in range(2)]
        den = [k.sb(f"m_den{i}", [128, 1], F32, st=s2) for i in range(2)]
        for d in range(2):
            k.memset(S32(None), 0.0)
            k.memset(Sbf(None), 0.0)
            for j, n in enumerate(self.dir_order(d)):
                i = j % 2
                tsl = slice(n * 128, n * 128 + 128)
                k.ts(vh[i](None), v1(None, slice(None), n, slice(None)), vsc(None, slice(None), d, slice(n, n + 1)), ALU.mult)
                b0 = self.bank[i]
                k.mm(b0(None, slice(None), slice(0, 128)), kT(None, slice(None), tsl), qT(None, slice(None), tsl))
                k.tt(Sm[i](None), b0(None, slice(None), slice(0, 128)), self.cmaskb(None, slice(None), d, slice(None)), ALU.mult)
                b1 = self.bank[2 + i]
                k.mm(b1(None, slice(None), slice(0, 65)), Sm[i](None), vh[i](None), start=True, stop=False)
                k.mm(b1(None, slice(None), slice(0, 65)), qT(None, slice(None), tsl), Sbf(None), start=False, stop=True)
                b2 = self.bank[4 + i]
                k.mm(b2(None, slice(0, 64), slice(0, 65)), ktok(None, slice(None), n, slice(None)), vh[i](None))
                rs_ = rowsc(None, slice(None), d, slice(n, n + 1))
                k.actf(den[i](None), b1(None, slice(None), slice(64, 65)), AF.Abs, scale=rs_)
                k.ts(den[i](None), den[i](None), 1.0, ALU.max)
                k.recip(den[i](None), den[i](None))
                k.tt(den[i](None), den[i](None), rs_, ALU.mult)
                if d == 0:
                    k.ts(oacc(None, slice(None), n, slice(None)), b1(None, slice(None), slice(0, 64)), den[i](None), ALU.mult)
                else:
                    k.stt(oacc(None, slice(None), n, slice(None)), b1(None, slice(None), slice(0, 64)), den[i](None),
                          oacc(None, slice(None), n, slice(None)), ALU.mult, ALU.add)
                k.tt(S32(None), S32(None), b2(None, slice(0, 64), slice(0, 65)), ALU.add)
                k.ts(S32(None), S32(None), csc(None, slice(None), d, slice(n, n + 1)), ALU.mult)
                k.cp(Sbf(None), S32(None), eng=k.act)
        self.head_norm_out(s2, oacc, gate, O["mnorm"], 0)


def core_cols(hd):
    r = lambda a, n: list(range(a, a + n))
    m = r(hd * 64, 64) + r(256 + hd * 64, 64) + r(512 + hd * 64, 64) + r(768 + hd * 64, 64) + r(1024 + hd * 64, 64) \
        + [1280 + hd, 1284 + hd, 1288 + hd, 1292 + hd]
    a0 = 1296
    a = r(a0 + hd * 64, 64) + r(a0 + 256 + hd * 64, 64) + r(a0 + 512 + hd * 64, 64) + r(a0 + 768 + hd * 64, 64)
    g0 = 2320
    g = r(g0 + hd * 32, 32) + r(g0 + 128 + hd * 32, 32) + r(g0 + 256 + hd * 64, 64) + r(g0 + 512 + hd * 64, 64) + r(g0 + 768, 32)
    d0 = 3120
    dd = r(d0 + hd * 64, 64) + r(d0 + 256 + hd * 64, 64) + r(d0 + 512 + hd * 64, 64) + r(d0 + 768 + hd * 64, 64) \
        + [d0 + 1024 + hd, d0 + 1028 + hd, d0 + 1032 + hd, d0 + 1036 + hd]
    cols = m + a + g + dd
    assert len(cols) == WCOLS
    return np.array(cols)


def rope_tables(n_lat):
    GRID_W = 64
    t = np.arange(n_lat)
    row = (t // GRID_W).astype(np.float32)
    col = (t % GRID_W).astype(np.float32)
    half = 16
    inv = np.power(np.float32(10000.0), -np.arange(0, half, 2, dtype=np.float32) / np.float32(half)).astype(np.float32)

    def tab(p):
        ang = (p[:, None] * inv).astype(np.float32)
        ang = np.concatenate([ang, ang], -1)
        return np.cos(ang).astype(np.float32), np.sin(ang).astype(np.float32)
    cr, sr = tab(row)
    cc, sc = tab(col)
    cos = np.concatenate([cr, cc], -1)
    sin = np.concatenate([sr, sc], -1)
    sgn = np.tile(np.concatenate([-np.ones(8), np.ones(8)]), 2).astype(np.float32)
    return np.ascontiguousarray(np.stack([cos, sin * sgn], axis=1).astype(np.float32))


def pack_layer(inp, l, b, hd, n_lat):
    H = 4
    P = Prog.SPO
    sp = np.zeros((1, 512), np.float32)
    def put(o, v):
        v = np.asarray(v, np.float32).reshape(-1)
        sp[0, o:o + v.size] = v
    put(P["mbi"], inp["mlstm_b_i"][l][:, hd])
    put(P["mbf"], inp["mlstm_b_f"][l][:, hd])
    put(P["ddt"], inp["gdn_dt_bias"][l][:, hd])
    put(P["dal"], inp["gdn_a_log"][l][:, hd])
    put(P["mnorm"], inp["mlstm_norm"][l][hd * 64:hd * 64 + 64])
    put(P["anorm"], inp["diff_norm"][l][hd * 64:hd * 64 + 64])
    put(P["gnorm"], inp["gla_norm"][l][hd * 64:hd * 64 + 64])
    put(P["dnorm"], inp["gdn_norm"][l][hd * 64:hd * 64 + 64])
    put(P["qn"], inp["diff_q_norm"][l])
    put(P["kn"], inp["diff_k_norm"][l])
    put(P["lam"], inp["diff_lambda"][l])
    gb = inp["gla_b"][l][:, hd * 32:hd * 32 + 32]
    sp2 = np.zeros((1, 128), np.float32)
    cols = core_cols(hd)
    d = {
        "wss": np.ascontiguousarray(inp["w_mod"][l][:, :2 * D]),
        "bss": np.ascontiguousarray(inp["b_mod"][l][:2 * D]),
        "normw": np.ascontiguousarray(inp["norm_w"][l]),
        "wcore": np.ascontiguousarray(inp["w_in"][l][:, cols]),
        "sp_row": sp,
        "gb_row": np.ascontiguousarray(gb.reshape(1, 64)),
        "convw": np.ascontiguousarray(np.concatenate(
            [inp["gdn_conv"][l][:, o + hd * 64:o + hd * 64 + 64] for o in (0, 256, 512)], axis=1)),
        "wup": np.ascontiguousarray(inp["gla_w_up"][l][:, :, hd * 32:hd * 32 + 32]),
        "rope": rope_tables(n_lat),
    }
    return d


def pack_pro(inp, l):
    return {
        "wout": np.ascontiguousarray(inp["w_out"][l]),
        "wgt": np.ascontiguousarray(inp["w_mod"][l][:, 2 * D:]),
        "bgt": np.ascontiguousarray(inp["b_mod"][l][2 * D:].reshape(1, D)),
    }


def _mix_attn(self, s2):
    k = self.k
    NT, T = self.NT, self.T
    NL = NT - 2
    O = self.SPO
    lam_init = float(self.cfg["lam_init"])
    qT = k.sb("a_qT", [64, T], BF16, st=s2)
    kT = k.sb("a_kT", [64, T], BF16, st=s2)
    v1 = k.sb("a_v1", [128, NT, 65], BF16, st=s2)
    gate = k.sb("a_gate", [128, NT, 64], F32, st=s2)
    oacc = k.sb("a_oacc", [128, NT, 64], F32, st=s2)
    rope = k.sb("a_rope", [128, NL, 2, 32], F32, st=s2)
    wn = k.sb("a_wn", [128, 4, 32], F32, st=s2)
    lam = k.sb("a_lam", [128, 1], F32, st=s2)
    k.dma(k.sp, self.cst, rope(None), self.dr["rope"].rearrange("(n p) a c -> p n a c", p=128))
    k.memset(v1(None, slice(None), slice(None), slice(64, 65)), 1.0)
    for g in range(4):
        o = O["qn"] if g < 2 else O["kn"]
        k.ts(wn(None, slice(None), g, slice(None)), self.sprep(None, slice(None), slice(o, o + 32)),
             (32.0 ** -0.5) if g < 2 else 1.0, ALU.mult)
    lt = k.sb("a_lt", [128, 2, 32], F32, st=s2)
    ls = k.sb("a_ls", [128, 2], F32, st=s2)
    lv = Ref(self.sprep, None, self.sprep.t[:, O["lam"]:O["lam"] + 128].rearrange("p (a b c) -> p a b c", a=2, b=2))
    k.tt(lt(None), Ref(self.sprep, None, lv.ap[:, :, 0, :]), Ref(self.sprep, None, lv.ap[:, :, 1, :]), ALU.mult)
    k.red(ls(None), lt(None))
    k.actf(ls(None), ls(None), AF.Exp)
    k.tt(lam(None), ls(None, slice(None), slice(0, 1)), ls(None, slice(None), slice(1, 2)), ALU.subtract)
    k.ts(lam(None), lam(None), lam_init, ALU.add)

    qk = [k.sb(f"a_qk{i}", [128, 4, 32], F32, st=s2) for i in range(2)]
    sq = [k.sb(f"a_sq{i}", [128, 4, 32], F32, st=s2) for i in range(2)]
    tb = [k.sb(f"a_tb{i}", [128, 4, 32], F32, st=s2) for i in range(2)]
    ss = [k.sb(f"a_ss{i}", [128, 4], F32, st=s2) for i in range(2)]
    qkb = [k.sb(f"a_qkb{i}", [128, 128], BF16, st=s2) for i in range(2)]

    import os
    LV = int(os.environ.get("ATTN_LV", "9"))

    def tok_evac(n, bk):
        i = n % 2
        if LV < 9:
            k.actf(gate(None, slice(None), n, slice(None)), bk(None, slice(None), slice(192, 256)), AF.Silu)
            if LV == 0:
                return
        src = Ref(bk, None, bk.t[:, 0:128].rearrange("p (g c) -> p g c", g=4))
        SUB = int(os.environ.get("ATTN_SUB", "9"))
        k.cp(qk[i](None), src, eng=k.act)
        if SUB == 0:
            return
        k.tt(sq[i](None), qk[i](None), qk[i](None), ALU.mult)
        if SUB == 1:
            return
        k.red(ss[i](None), sq[i](None))
        if SUB == 2:
            return
        k.actf(ss[i](None), ss[i](None), AF.Sqrt, bias=self.epsc(None), scale=1.0 / 32)
        if SUB == 3:
            return
        k.recip(ss[i](None), ss[i](None))
        if SUB == 4:
            return
        k.tt(sq[i](None), qk[i](None), Ref(ss[i], None, ss[i].t[:].unsqueeze(2).broadcast_to([128, 4, 32])), ALU.mult)
        if SUB == 5:
            return
        k.tt(qk[i](None), sq[i](None), wn(None), ALU.mult)
        if LV == 1:
            return
        if n >= 2 and LV != 3:
            cos = Ref(rope, None, rope.t[:, n - 2, 0, :].unsqueeze(1).broadcast_to([128, 4, 32]))
            k.tt(sq[i](None), qk[i](None), cos, ALU.mult)
            x5 = qk[i].t[:].rearrange("p g (a h c) -> p g a h c", a=2, h=2)
            t5 = tb[i].t[:].rearrange("p g (a h c) -> p g a h c", a=2, h=2)
            s5 = rope.t[:, n - 2, 1, :].rearrange("p (a h c) -> p a h c", a=2, h=2)
            for h in range(2):
                k.tt(Ref(tb[i], None, t5[:, :, :, h, :]), Ref(qk[i], None, x5[:, :, :, 1 - h, :]),
                     Ref(rope, None, s5[:, :, h, :].unsqueeze(1).broadcast_to([128, 4, 2, 8])), ALU.mult)
            k.tt(qkb[i](None), Ref(sq[i], None, sq[i].t[:].rearrange("p g c -> p (g c)")),
                 Ref(tb[i], None, tb[i].t[:].rearrange("p g c -> p (g c)")), ALU.add)
        else:
            k.cp(qkb[i](None), Ref(qk[i], None, qk[i].t[:].rearrange("p g c -> p (g c)")))
        if LV == 2:
            return
        pb = self.bank[6 + i]
        pbv = pb.t[:].bitcast(BF16)
        k.tr(Ref(pb, None, pbv[0:64, 0:128]), qkb[i](None, slice(None), slice(0, 64)), self.identb(None))
        k.tr(Ref(pb, None, pbv[0:64, 128:256]), qkb[i](None, slice(None), slice(64, 128)), self.identb(None))
        k.cp(qT(None, slice(None), slice(n * 128, n * 128 + 128)), Ref(pb, None, pbv[0:64, 0:128]), eng=k.act)
        k.cp(kT(None, slice(None), slice(n * 128, n * 128 + 128)), Ref(pb, None, pbv[0:64, 128:256]))
        k.cp(v1(None, slice(None), n, slice(0, 64)), bk(None, slice(None), slice(128, 192)), eng=k.act)
        k.actf(gate(None, slice(None), n, slice(None)), bk(None, slice(None), slice(192, 256)), AF.Silu)

    with contextlib.ExitStack() as s3:
        self.proj_pass(s3, MC, AC, tok_evac, [], None)
        k.barrier()

    import os
    if os.environ.get("ATTN_STOP") == "1":
        k.memset(oacc(None), 1.0)
        self.head_norm_out(s2, oacc, gate, O["anorm"], 1, extra_scale=(1.0 - lam_init))
        return
    Pt = [k.sb(f"a_P{i}", [128, 512], BF16, st=s2) for i in range(2)]
    Oj = [k.sb(f"a_O{i}", [128, 4, 65], F32, st=s2) for i in range(2)]
    rj = [k.sb(f"a_r{i}", [128, 4, 1], F32, st=s2) for i in range(2)]
    tmo = k.sb("a_tmo", [128, 4, 64], F32, st=s2)
    blocks = []
    if self.cfg.get("need_ctx", True):
        blocks.append((0, 2, [0, 1]))
    n = 2
    while n < NT:
        nq = min(4, NT - n)
        blocks.append((n, nq, list(range(NT))))
        n += nq
    cnt = 0
    for (t0, nq, keys) in blocks:
        ntok = nq * 128
        for j in range(2):
            ps = slice(32 * j, 32 * j + 32)
            for ki, kt in enumerate(keys):
                sb_ = self.bank[cnt % 2]
                P = Pt[cnt % 2]
                cnt += 1
                k.mm(sb_(None, slice(None), slice(0, ntok)), kT(None, ps, slice(kt * 128, kt * 128 + 128)),
                     qT(None, ps, slice(t0 * 128, t0 * 128 + ntok)))
                k.actf(P(None, slice(None), slice(0, ntok)), sb_(None, slice(None), slice(0, ntok)), AF.Exp)
                for qs in range(nq):
                    k.mm(self.bank[2 + qs](None, slice(None), slice(0, 65)), P(None, slice(None), slice(qs * 128, qs * 128 + 128)),
                         v1(None, slice(None), kt, slice(None)), start=(ki == 0), stop=(ki == len(keys) - 1))
            for qs in range(nq):
                k.cp(Oj[j](None, slice(None), qs, slice(None)), self.bank[2 + qs](None, slice(None), slice(0, 65)),
                     eng=(k.dve if qs % 2 else k.act))
            k.recip(rj[j](None), Oj[j](None, slice(None), slice(None), slice(64, 65)))
        k.ts(rj[1](None), rj[1](None), lam(None), ALU.mult)
        osl = oacc(None, slice(None), slice(t0, t0 + nq), slice(None))
        k.tt(osl, Oj[0](None, slice(None), slice(0, nq), slice(0, 64)),
             Ref(rj[0], None, rj[0].t[:, 0:nq, :].broadcast_to([128, nq, 64])), ALU.mult)
        k.tt(tmo(None, slice(None), slice(0, nq), slice(None)), Oj[1](None, slice(None), slice(0, nq), slice(0, 64)),
             Ref(rj[1], None, rj[1].t[:, 0:nq, :].broadcast_to([128, nq, 64])), ALU.mult)
        k.tt(osl, osl, tmo(None, slice(None), slice(0, nq), slice(None)), ALU.subtract)
    if not self.cfg.get("need_ctx", True):
        k.memset(oacc(None, slice(None), slice(0, 2), slice(None)), 1.0)
    self.head_norm_out(s2, oacc, gate, O["anorm"], 1, extra_scale=(1.0 - lam_init))


Prog.mix_attn = _mix_attn


def _mix_gla(self, s2):
    k = self.k
    NT, T = self.NT, self.T
    O = self.SPO
    c0 = MC + AC
    qtok = k.sb("g_qtok", [128, NT, 32], F32, st=s2)
    ktok = k.sb("g_ktok", [128, NT, 32], F32, st=s2)
    vtok = k.sb("g_vtok", [128, NT, 64], BF16, st=s2)
    gate = k.sb("g_gate", [128, NT, 64], F32, st=s2)
    oacc = k.sb("g_oacc", [128, NT, 64], F32, st=s2)
    loga = [k.sb(f"g_loga{d}", [128, NT, 32], F32, st=s2) for d in range(2)]
    gbrep = k.sb("g_gbrep", [128, 2, 32], F32, st=s2)
    wup = k.sb("g_wup", [16, 2, 32], F32, st=s2)
    rTg = [k.sb(f"g_rTg{i}", [16, 512], F32, st=s2) for i in range(2)]
    gtmp = [k.sb(f"g_gtmp{i}", [128, 4, 32], F32, st=s2) for i in range(2)]
    k.dma(k.sp, self.cst, Ref(gbrep, None, gbrep.t[:].rearrange("p a c -> p (a c)")),
          self.dr["gb_row"].partition_broadcast(128).rearrange("p o n -> p (o n)"))
    k.dma(k.sp, self.cst, wup(None), self.dr["wup"].rearrange("d r c -> r d c"))

    def tok_evac(n, bk):
        k.actf(qtok(None, slice(None), n, slice(None)), bk(None, slice(None), slice(0, 32)), AF.Copy, scale=32.0 ** -0.5)
        k.cp(ktok(None, slice(None), n, slice(None)), bk(None, slice(None), slice(32, 64)), eng=k.act)
        k.cp(vtok(None, slice(None), n, slice(None)), bk(None, slice(None), slice(64, 128)), eng=k.act)
        k.actf(gate(None, slice(None), n, slice(None)), bk(None, slice(None), slice(128, 192)), AF.Silu)

    def d_evac(d, t0, ntok, bk):
        nt = ntok // 128
        n0 = t0 // 128
        k.cp(rTg[d](None, slice(None), slice(0, ntok)), bk(None, slice(0, 16), slice(0, ntok)))
        for ti in range(nt):
            k.mm(bk(None, slice(None), slice(ti * 32, ti * 32 + 32)), rTg[d](None, slice(None), slice(ti * 128, ti * 128 + 128)),
                 wup(None, slice(None), d, slice(None)))
        g = gtmp[d]
        gs = g(None, slice(None), slice(0, nt), slice(None))
        k.tt(gs, Ref(bk, None, bk.t[:, 0:nt * 32].rearrange("p (a c) -> p a c", c=32)),
             Ref(gbrep, None, gbrep.t[:, d, :].unsqueeze(1).broadcast_to([128, nt, 32])), ALU.add)
        k.actf(gs, gs, AF.Exp, scale=-1.0)
        k.actf(gs, gs, AF.Ln, bias=1.0, scale=1.0)
        k.ts(loga[d](None, slice(None), slice(n0, n0 + nt), slice(None)), gs, -1.0 / 16.0, ALU.mult)

    with contextlib.ExitStack() as s3:
        self.proj_pass(s3, c0, GC, tok_evac, [(c0 + 192, 16), (c0 + 208, 16)], d_evac)
        k.barrier()

    qkg = k.sb("g_qkg", [128, NT, 64], BF16, st=s2)
    qkT = k.sb("g_qkT", [32, NT, 256], BF16, st=s2)
    csc = k.sb("g_csc", [32, NT], F32, st=s2)
    eb = k.sb("g_eb", [128, 16, 32], F32, st=s2)
    S32 = k.sb("g_S32", [32, 64], F32, st=s2)
    Sbf = k.sb("g_Sbf", [32, 64], BF16, st=s2)
    Sm = [k.sb(f"g_Sm{i}", [128, 128], BF16, st=s2) for i in range(2)]
    for d in range(2):
        n0 = 0
        while n0 < NT:
            nt = min(16, NT - n0)
            bk = self.bank[0]
            k.mm(bk(None, slice(None), slice(0, nt * 32)), self.cmask(None, slice(None), d, slice(None)),
                 Ref(loga[d], None, loga[d].t[:, n0:n0 + nt, :].rearrange("p a c -> p (a c)")))
            bv = Ref(bk, None, bk.t[:, 0:nt * 32].rearrange("p (a c) -> p a c", c=32))
            es = eb(None, slice(None), slice(0, nt), slice(None))
            k.actf(es, bv, AF.Exp)
            k.tt(qkg(None, slice(None), slice(n0, n0 + nt), slice(0, 32)), qtok(None, slice(None), slice(n0, n0 + nt), slice(None)),
                 es, ALU.mult)
            k.actf(es, bv, AF.Exp, scale=-1.0)
            k.tt(qkg(None, slice(None), slice(n0, n0 + nt), slice(32, 64)), ktok(None, slice(None), slice(n0, n0 + nt), slice(None)),
                 es, ALU.mult)
            n0 += nt
        bk = self.bank[1]
        for n in range(NT):
            k.mm(bk(None, slice(0, 32), slice(2 * n, 2 * n + 2)), loga[d](None, slice(None), n, slice(None)),
                 self.ones(None, slice(None), slice(0, 2)))
        k.actf(csc(None), Ref(bk, None, bk.t[0:32, 0:2 * NT].rearrange("p (n a) -> p n a", a=2)[:, :, 0]), AF.Exp)
        for n in range(NT):
            pb = self.bank[2 + n % 2]
            pbv = pb.t[:].bitcast(BF16)
            k.tr(Ref(pb, None, pbv[0:32, 0:128]), qkg(None, slice(None), n, slice(0, 32)), self.identb(None))
            k.tr(Ref(pb, None, pbv[0:32, 128:256]), qkg(None, slice(None), n, slice(32, 64)), self.identb(None))
            k.cp(qkT(None, slice(None), n, slice(None)), Ref(pb, None, pbv[0:32, 0:256]), eng=(k.act if n % 2 else k.dve))
        k.memset(S32(None), 0.0)
        k.memset(Sbf(None), 0.0)
        for j, n in enumerate(self.dir_order(d)):
            i = j % 2
            qT_ = qkT(None, slice(None), n, slice(0, 128))
            kT_ = qkT(None, slice(None), n, slice(128, 256))
            b0 = self.bank[4 + i]
            k.mm(b0(None, slice(None), slice(0, 128)), kT_, qT_)
            k.tt(Sm[i](None), b0(None, slice(None), slice(0, 128)), self.cmaskb(None, slice(None), d, slice(None)), ALU.mult)
            b1 = self.bank[6 + i]
            k.mm(b1(None, slice(None), slice(0, 64)), Sm[i](None), vtok(None, slice(None), n, slice(None)), start=True, stop=False)
            k.mm(b1(None, slice(None), slice(0, 64)), qT_, Sbf(None), start=False, stop=True)
            b2 = self.bank[i]
            k.mm(b2(None, slice(0, 32), slice(256, 320)), qkg(None, slice(None), n, slice(32, 64)), vtok(None, slice(None), n, slice(None)))
            if d == 0:
                k.cp(oacc(None, slice(None), n, slice(None)), b1(None, slice(None), slice(0, 64)), eng=k.act)
            else:
                k.tt(oacc(None, slice(None), n, slice(None)), b1(None, slice(None), slice(0, 64)),
                     oacc(None, slice(None), n, slice(None)), ALU.add)
            k.tt(S32(None), S32(None), b2(None, slice(0, 32), slice(256, 320)), ALU.add)
            k.ts(S32(None), S32(None), csc(None, slice(None), slice(n, n + 1)), ALU.mult)
            k.cp(Sbf(None), S32(None), eng=k.act)
    self.head_norm_out(s2, oacc, gate, O["gnorm"], 2)


Prog.mix_gla = _mix_gla


def _mix_gdn(self, s2):
    k = self.k
    NT, T = self.NT, self.T
    O = self.SPO
    c0 = MC + AC + GC
    TP = T + 8
    gate = k.sb("d_gate", [128, NT, 64], F32, st=s2)
    graw = k.sb("d_graw", [128, NT, 4], F32, st=s2)
    kh = k.sb("d_kh", [128, NT, 64], F32, st=s2)
    vtok = k.sb("d_vtok", [128, NT, 64], F32, st=s2)
    qT = k.sb("d_qT", [64, T], BF16, st=s2)
    kT = k.sb("d_kT", [64, T], BF16, st=s2)
    cwA = k.sb("d_cwA", [128, 5], F32, st=s2)
    cwB = k.sb("d_cwB", [64, 5], F32, st=s2)
    k.dma(k.sp, self.cst, cwA(None), self.dr["convw"][:, 0:128].rearrange("j c -> c j"), allow_slow_non_contiguous=True)
    k.dma(k.sp, self.cst, cwB(None), self.dr["convw"][:, 128:192].rearrange("j c -> c j"), allow_slow_non_contiguous=True)

    def pcol(t0):
        return t0 + 2 if t0 < 256 else t0 + 6

    with contextlib.ExitStack() as s3:
        rawA = k.sb("d_rawA", [128, TP], F32, st=s3)
        rawB = k.sb("d_rawB", [64, TP], F32, st=s3)
        k.memset(rawA(None), 0.0, eng=k.pool)
        k.memset(rawB(None), 0.0, eng=k.pool)

        def tok_evac(n, bk):
            k.actf(gate(None, slice(None), n, slice(None)), bk(None, slice(None), slice(0, 64)), AF.Silu)
            k.cp(graw(None, slice(None), n, slice(None)), bk(None, slice(None), slice(64, 68)), eng=k.act)

        def d_evac(di, t0, ntok, bk):
            pc = pcol(t0)
            if di == 0:
                k.cp(rawA(None, slice(None), slice(pc, pc + ntok)), bk(None, slice(None), slice(0, ntok)))
            else:
                k.cp(rawB(None, slice(None), slice(pc, pc + ntok)), bk(None, slice(0, 64), slice(0, ntok)), eng=k.act)

        with contextlib.ExitStack() as s4:
            self.proj_pass(s4, c0 + 192, 68, tok_evac, [(c0, 128), (c0 + 128, 64)], d_evac)
            k.barrier()
        CB = 1024
        cvA = [k.sb(f"d_cvA{i}", [128, CB], F32, st=s3) for i in range(2)]
        cvB = [k.sb(f"d_cvB{i}", [64, CB], F32, st=s3) for i in range(2)]
        qkv = [k.sb(f"d_qkv{i}", [128, 3, 64], F32, st=s3) for i in range(2)]
        sq = [k.sb(f"d_sq{i}", [128, 2, 64], F32, st=s3) for i in range(2)]
        ss = [k.sb(f"d_ss{i}", [128, 2], F32, st=s3) for i in range(2)]
        qkb = [k.sb(f"d_qkb{i}", [128, 2, 64], BF16, st=s3) for i in range(2)]
        blocks = [(0, 256)]
        t0 = 256
        while t0 < T:
            nb = min(CB, T - t0)
            blocks.append((t0, nb))
            t0 += nb
        for bi, (t0, nb) in enumerate(blocks):
            pc = pcol(t0)
            ca, cb_ = cvA[bi % 2], cvB[bi % 2]
            for (cv, raw, cw, P) in ((ca, rawA, cwA, 128), (cb_, rawB, cwB, 64)):
                o = cv(None, slice(None), slice(0, nb))
                k.ts(o, raw(None, slice(None), slice(pc - 2, pc - 2 + nb)), cw(None, slice(None), slice(0, 1)), ALU.mult)
                for j in range(1, 5):
                    k.stt(o, raw(None, slice(None), slice(pc - 2 + j, pc - 2 + j + nb)), cw(None, slice(None), slice(j, j + 1)),
                          o, ALU.mult, ALU.add)
                k.actf(o, o, AF.Silu)
            for ti in range(nb // 128):
                n = t0 // 128 + ti
                i = n % 2
                pb = self.bank[i]
                k.tr(pb(None, slice(None), slice(0, 128)), ca(None, slice(None), slice(ti * 128, ti * 128 + 128)), self.ident(None))
                k.tr(pb(None, slice(None), slice(128, 192)), cb_(None, slice(None), slice(ti * 128, ti * 128 + 128)),
                     self.ident(None, slice(0, 64), slice(0, 64)))
                k.cp(qkv[i](None), Ref(pb, None, pb.t[:, 0:192].rearrange("p (a c) -> p a c", a=3)), eng=k.act)
                k.cp(vtok(None, slice(None), n, slice(None)), qkv[i](None, slice(None), 2, slice(None)), eng=k.pool)
                qk2 = qkv[i](None, slice(None), slice(0, 2), slice(None))
                k.tt(sq[i](None), qk2, qk2, ALU.mult)
                k.red(ss[i](None), sq[i](None))
                k.actf(ss[i](None), ss[i](None), AF.Sqrt, bias=self.epsc(None), scale=1.0)
                k.recip(ss[i](None), ss[i](None))
                k.ts(ss[i](None, slice(None), slice(0, 1)), ss[i](None, slice(None), slice(0, 1)), 0.125, ALU.mult)
                k.tt(sq[i](None), qk2, Ref(ss[i], None, ss[i].t[:].unsqueeze(2).broadcast_to([128, 2, 64])), ALU.mult)
                k.cp(kh(None, slice(None), n, slice(None)), sq[i](None, slice(None), 1, slice(None)), eng=k.pool)
                k.cp(qkb[i](None), sq[i](None))
                pb2 = self.bank[2 + i]
                pbv = pb2.t[:].bitcast(BF16)
                k.tr(Ref(pb2, None, pbv[0:64, 0:128]), qkb[i](None, slice(None), 0, slice(None)), self.identb(None))
                k.tr(Ref(pb2, None, pbv[0:64, 128:256]), qkb[i](None, slice(None), 1, slice(None)), self.identb(None))
                k.cp(qT(None, slice(None), slice(n * 128, n * 128 + 128)), Ref(pb2, None, pbv[0:64, 0:128]), eng=k.act)
                k.cp(kT(None, slice(None), slice(n * 128, n * 128 + 128)), Ref(pb2, None, pbv[0:64, 128:256]))
        k.barrier()

    oacc = k.sb("d_oacc", [128, NT, 64], F32, st=s2)
    beta = k.sb("d_beta", [128, NT], F32, st=s2)
    gg = k.sb("d_g", [128, NT], F32, st=s2)
    gc = k.sb("d_gc", [128, NT], F32, st=s2)
    egc = k.sb("d_egc", [128, NT], F32, st=s2)
    kds = k.sb("d_kds", [128, NT], F32, st=s2)
    sdec = k.sb("d_sdec", [64, 2, NT], F32, st=s2)
    nA = k.sb("d_nA", [128, 1], F32, st=s2)
    sameb = k.sb("d_sameb", [128, 128], F32, st=s2)
    negm = k.sb("d_negm", [128, 2, 128], F32, st=s2)
    k.tt(sameb(None), self.cmask(None, slice(None), 2, slice(None)), self.cmask(None, slice(None), 4, slice(None)), ALU.add)
    k.ts(negm(None, slice(None), 0, slice(None)), self.cmask(None, slice(None), 4, slice(None)), -1.0, ALU.mult)
    k.ts(negm(None, slice(None), 1, slice(None)), self.cmask(None, slice(None), 5, slice(None)), -1.0, ALU.mult)
    dg = k.sb("d_dg", [128, 128], F32, st=s2)
    z1 = k.sb("d_z1", [128, 128], F32, st=s2)
    E1 = k.sb("d_E1", [128, 128], F32, st=s2)
    E2 = k.sb("d_E2", [128, 128], F32, st=s2)
    tmp = k.sb("d_tmp", [128, 128], F32, st=s2)
    Pm = [k.sb(f"d_P{i}", [128, 128], F32, st=s2) for i in range(2)]
    PTm = [k.sb(f"d_PT{i}", [128, 128], F32, st=s2) for i in range(2)]
    X = k.sb("d_X", [128, 128], F32, st=s2)
    W0 = k.sb("d_W0", [128, 64], F32, st=s2)
    V0 = k.sb("d_V0", [128, 64], F32, st=s2)
    kdec = k.sb("d_kdec", [128, 64], BF16, st=s2)
    u_s = k.sb("d_u", [128, 64], F32, st=s2)
    wT = k.sb("d_wT", [64, 128], BF16, st=s2)
    AT = k.sb("d_AT", [128, 128], BF16, st=s2)
    vnew = k.sb("d_vnew", [128, 64], BF16, st=s2)
    ps2 = k.sb("d_ps2", [128, 64], F32, st=s2)
    otmp = k.sb("d_otmp", [128, 64], F32, st=s2)
    sc1 = k.sb("d_sc1", [128, 1], F32, st=s2)
    S32 = k.sb("d_S32", [64, 64], F32, st=s2)
    Sbf = k.sb("d_Sbf", [64, 64], BF16, st=s2)
    B = self.bank
    for d in range(2):
        k.actf(beta(None), graw(None, slice(None), slice(None), d), AF.Sigmoid)
        k.actf(gg(None), graw(None, slice(None), slice(None), 2 + d), AF.Exp,
               bias=self.sprep(None, slice(None), slice(O["ddt"] + d, O["ddt"] + d + 1)), scale=1.0)
        k.actf(gg(None), gg(None), AF.Ln, bias=1.0, scale=1.0)
        k.actf(nA(None), self.sprep(None, slice(None), slice(O["dal"] + d, O["dal"] + d + 1)), AF.Exp)
        k.ts(gg(None), gg(None), nA(None), ALU.mult, -1.0, ALU.mult)
        k.mm(B[0](None, slice(None), slice(0, NT)), self.cmask(None, slice(None), 2 + d, slice(None)), gg(None))
        k.mm(B[0](None, slice(None), slice(128, 128 + NT)), sameb(None), gg(None))
        for c in range(2):
            k.mm(B[1](None, slice(0, 64), slice(c * 128, c * 128 + NT)), sameb(None, slice(None), slice(c * 64, c * 64 + 64)), gg(None))
        k.cp(gc(None), B[0](None, slice(None), slice(0, NT)))
        k.actf(egc(None), B[0](None, slice(None), slice(0, NT)), AF.Exp)
        k.tt(kds(None), B[0](None, slice(None), slice(128, 128 + NT)), gc(None), ALU.subtract)
        k.actf(kds(None), kds(None), AF.Exp)
        for c in range(2):
            k.actf(sdec(None, slice(None), c, slice(None)), B[1](None, slice(0, 64), slice(c * 128, c * 128 + NT)), AF.Exp)
        k.memset(S32(None), 0.0)
        k.memset(Sbf(None), 0.0)
        for j, n in enumerate(self.dir_order(d)):
            tsl = slice(n * 128, n * 128 + 128)
            gcn = gc(None, slice(None), slice(n, n + 1))
            btn = beta(None, slice(None), slice(n, n + 1))
            k.ts(dg(None), self.ident(None), gcn, ALU.mult)
            k.mm(B[1](None, slice(None), slice(0, 128)), self.ones(None), dg(None))
            k.ts(z1(None), B[1](None, slice(None), slice(0, 128)), gcn, ALU.subtract, 0.0, ALU.max)
            k.actf(E1(None), z1(None), AF.Exp, scale=-1.0)
            k.ts(z1(None), B[1](None, slice(None), slice(0, 128)), gcn, ALU.subtract, 0.0, ALU.min)
            k.actf(E2(None), z1(None), AF.Exp)
            k.tt(E2(None), E2(None), self.cmask(None, slice(None), 2 + d, slice(None)), ALU.mult, eng=k.pool)
            k.mm(B[0](None, slice(None), slice(0, 128)), kT(None, slice(None), tsl), kT(None, slice(None), tsl))
            k.tt(tmp(None), B[0](None, slice(None), slice(0, 128)), E1(None), ALU.mult)
            k.stt(Pm[0](None), tmp(None), btn, negm(None, slice(None), d, slice(None)), ALU.mult, ALU.mult)
            k.tr(B[2](None, slice(None), slice(0, 128)), Pm[0](None), self.ident(None))
            k.cp(PTm[0](None), B[2](None, slice(None), slice(0, 128)), eng=k.act)
            k.tt(X(None), B[2](None, slice(None), slice(0, 128)), self.ident(None), ALU.add)
            cur = 0
            for r in range(1, 6):
                nx = 1 - cur
                k.mm(B[2](None, slice(None), slice(0, 128)), PTm[cur](None), Pm[cur](None))
                if r < 5:
                    k.mm(B[3](None, slice(None), slice(0, 128)), Pm[cur](None), PTm[cur](None))
                k.cp(Pm[nx](None), B[2](None, slice(None), slice(0, 128)))
                if r < 5:
                    k.cp(PTm[nx](None), B[3](None, slice(None), slice(0, 128)), eng=k.act)
                k.mm(B[4](None, slice(None), slice(0, 128)), Pm[nx](None), X(None))
                k.tt(X(None), X(None), B[4](None, slice(None), slice(0, 128)), ALU.add)
                cur = nx
            k.ts(V0(None), vtok(None, slice(None), n, slice(None)), btn, ALU.mult, eng=k.pool)
            k.ts(sc1(None), btn, egc(None, slice(None), slice(n, n + 1)), ALU.mult, eng=k.pool)
            k.ts(W0(None), kh(None, slice(None), n, slice(None)), sc1(None), ALU.mult, eng=k.pool)
            k.ts(kdec(None), kh(None, slice(None), n, slice(None)), kds(None, slice(None), slice(n, n + 1)), ALU.mult, eng=k.pool)
            k.mm(B[5](None, slice(None), slice(0, 64)), X(None), V0(None))
            k.mm(B[5](None, slice(0, 64), slice(128, 256)), W0(None), X(None))
            k.cp(u_s(None), B[5](None, slice(None), slice(0, 64)))
            k.cp(wT(None), B[5](None, slice(0, 64), slice(128, 256)), eng=k.act)
            k.mm(B[6](None, slice(None), slice(0, 128)), kT(None, slice(None), tsl), qT(None, slice(None), tsl))
            k.tt(AT(None), B[6](None, slice(None), slice(0, 128)), E2(None), ALU.mult)
            for c in ([0, 1] if d == 0 else [1, 0]):
                pc = slice(64 * c, 64 * c + 64)
                tc_ = slice(n * 128 + 64 * c, n * 128 + 64 * c + 64)
                k.mm(B[7](None, pc, slice(0, 64)), wT(None, slice(None), pc), Sbf(None))
                k.tt(vnew(None, pc, slice(None)), u_s(None, pc, slice(None)), B[7](None, pc, slice(0, 64)), ALU.subtract)
                k.mm(B[3](None, pc, slice(0, 64)), qT(None, slice(None), tc_), Sbf(None))
                k.mm(B[3](None, pc, slice(64, 128)), AT(None, pc, pc), vnew(None, pc, slice(None)))
                k.mm(B[1](None, slice(0, 64), slice(256, 320)), kdec(None, pc, slice(None)), vnew(None, pc, slice(None)))
                k.cp(ps2(None, pc, slice(None)), B[3](None, pc, slice(64, 128)), eng=k.act)
                osl = oacc(None, pc, n, slice(None))
                if d == 0:
                    k.stt(osl, B[3](None, pc, slice(0, 64)), egc(None, pc, slice(n, n + 1)), ps2(None, pc, slice(None)),
                          ALU.mult, ALU.add)
                else:
                    k.stt(otmp(None, pc, slice(None)), B[3](None, pc, slice(0, 64)), egc(None, pc, slice(n, n + 1)),
                          ps2(None, pc, slice(None)), ALU.mult, ALU.add)
                    k.tt(osl, osl, otmp(None, pc, slice(None)), ALU.add, eng=k.pool)
                k.stt(S32(None), S32(None), sdec(None, slice(None), c, slice(n, n + 1)), B[1](None, slice(0, 64), slice(256, 320)),
                      ALU.mult, ALU.add)
                k.cp(Sbf(None), S32(None), eng=k.act)
    self.head_norm_out(s2, oacc, gate, O["dnorm"], 3)


Prog.mix_gdn = _mix_gdn


_PROG_CACHE = {}


def _prog(NT, pro, body, l):
    lam_init = 0.8 - 0.6 * math.exp(-0.3 * l)
    need_ctx = l < DEPTH - 1
    key = (NT, pro, body, l if body else -1)
    if key not in _PROG_CACHE:
        _PROG_CACHE[key] = Prog(dict(NT=NT, pro=pro, body=body, mixers=["mlstm", "attn", "gla", "gdn"],
                                     lam_init=lam_init, need_ctx=need_ctx))
    return _PROG_CACHE[key]


def _gather_y(res, T):
    out = []
    for b in range(2):
        yb = np.empty((T, D), np.float32)
        for hd in range(4):
            yc = res.results[b * 4 + hd]["y"]
            for g in range(4):
                yb[:, g * 256 + hd * 64:g * 256 + hd * 64 + 64] = yc[:, g * 64:g * 64 + 64]
        out.append(yb)
    return out


def kernel(**inputs):
    inp = {k: np.asarray(v) for k, v in inputs.items()}
    x = inp["x"]
    n_lat = x.shape[1]
    NT = 2 + n_lat // 128
    T = NT * 128
    consts = const_arrays()
    xs = [np.ascontiguousarray(np.concatenate([inp["ctx"][b], x[b]], 0)) for b in range(2)]
    c2 = [np.ascontiguousarray(np.stack([inp["c"][b], inp["c_ctx"]], 0)) for b in range(2)]
    ys = None
    for l in range(DEPTH + 1):
        pro = l > 0
        body = l < DEPTH
        P = _prog(NT, pro, body, l)
        maps = []
        for core in range(8):
            b, hd = core // 4, core % 4
            d = dict(consts)
            d["xin"] = xs[b]
            d["c2"] = c2[b]
            if pro:
                d.update(pack_pro(inp, l - 1))
                d["yprev"] = ys[b]
            if body:
                d.update(pack_layer(inp, l, b, hd, n_lat))
            maps.append({k: v for k, v in d.items() if k in P.dr})
        res = run_bass_kernel_spmd(P.nc, maps, core_ids=list(range(8)))
        if pro:
            xs = [np.ascontiguousarray(res.results[b * 4]["xout"]) for b in range(2)]
        if body:
            ys = _gather_y(res, T)
    out = np.stack([xs[b][256:] for b in range(2)], 0).astype(np.float32)
    return out
```
